# Optimizing a Trainium2 kernel written in Bass

```python
import jax, jax.numpy as jnp
from jax import lax
import numpy as np

D_MODEL = 1024
BATCH = 16
SEQ = 2048
DEPTH = 1

CHUNK = 64
QUERY_BLOCK = 128
HEAD_DIM = 64
SB_HEADS = 8
RW_HEADS = 8
BRANCH_WIDTH = 512
N_BRANCHES = 2
DECAY_LORA = 64
ICLR_LORA = 64
GATE_LORA = 160
D_FF = 2816
CONV_WIDTH = 3
NORM_EPS = 1e-6
GN_EPS = 64e-5

SB_COLS = 3 * BRANCH_WIDTH
RW_COLS = 3 * BRANCH_WIDTH + DECAY_LORA + ICLR_LORA + GATE_LORA
GATE_COLS = N_BRANCHES * D_MODEL
IN_COLS = SB_COLS + RW_COLS + GATE_COLS

kernel_name = "stickbreak_rwkv7_gated_hybrid"


def rmsnorm(x, g):
    x32 = x.astype(jnp.float32)
    y = x32 * lax.rsqrt(jnp.mean(x32 * x32, axis=-1, keepdims=True) + NORM_EPS) * g
    return y.astype(x.dtype)


def stick_breaking_attention(q, k, v):
    seq, dh = q.shape[2], q.shape[3]
    scale = dh ** -0.5
    outs = []
    for blk in range(seq // QUERY_BLOCK):
        q0 = blk * QUERY_BLOCK
        q1 = q0 + QUERY_BLOCK
        qb = q[:, :, q0:q1]
        kb = k[:, :, :q1]
        vb = v[:, :, :q1]
        z = jnp.einsum('bhqd,bhkd->bhqk', qb, kb) * scale
        t_pos = q0 + jnp.arange(QUERY_BLOCK)
        s_pos = jnp.arange(q1)
        strict = s_pos[None, :] < t_pos[:, None]
        log_keep = jnp.where(strict, jax.nn.log_sigmoid(-z), 0.0)
        after = lax.cumsum(log_keep, axis=3, reverse=True) - log_keep
        weights = jnp.where(strict, jnp.exp(jax.nn.log_sigmoid(z) + after), 0.0)
        outs.append(jnp.einsum('bhqk,bhkd->bhqd', weights, vb))
    return jnp.concatenate(outs, axis=2)


def token_shift(p, mu):
    prev = jnp.pad(p, ((0, 0), (1, 0), (0, 0)))[:, :-1]
    return p + (prev - p) * mu


def rwkv7_recurrence(r, w, k, v, a, b):
    bsz, _, h, n = r.shape

    def step(state, inp):
        r_t, w_t, k_t, v_t, a_t, b_t = inp
        sa = jnp.einsum('bhvk,bhk->bhv', state, a_t)
        state = (state * w_t[:, :, None, :] + sa[..., None] * b_t[:, :, None, :]
                 + v_t[..., None] * k_t[:, :, None, :])
        y = jnp.einsum('bhvk,bhk->bhv', state, r_t)
        return state, y

    xs = tuple(jnp.moveaxis(t, 1, 0) for t in (r, w, k, v, a, b))
    s0 = jnp.zeros((bsz, h, n, n), jnp.float32)
    _, ys = lax.scan(step, s0, xs)
    return jnp.moveaxis(ys, 0, 1)


def causal_depthwise_conv(u, w, bias):
    c = u.shape[-1]
    y = lax.conv_general_dilated(
        u, w[:, None, :].astype(u.dtype), window_strides=(1,),
        padding=[(CONV_WIDTH - 1, 0)], dimension_numbers=('NWC', 'WIO', 'NWC'),
        feature_group_count=c)
    return y + bias


def hybrid_layer(x, attn_norm_g, w_in, q_norm_g, k_norm_g, rwkv_shift_mu, decay_base, decay_up,
                 iclr_base, iclr_up, out_gate_up, key_k, key_a, bonus_rk, group_norm_w,
                 group_norm_b, branch_gate_b, w_branch, w_out, ffn_norm_g, w_ffn_up,
                 ffn_conv_w, ffn_conv_b, w_ffn_down):
    bsz, seq, _ = x.shape
    f32 = jnp.float32
    heads = lambda t: t.reshape(bsz, seq, -1, HEAD_DIM)

    h = rmsnorm(x, attn_norm_g)
    proj = h @ w_in
    p_sb, p_rw, p_gate = jnp.split(proj, [SB_COLS, SB_COLS + RW_COLS], axis=-1)

    q_sb, k_sb, v_sb = jnp.split(p_sb, 3, axis=-1)
    q_sb = rmsnorm(heads(q_sb), q_norm_g).astype(f32).transpose(0, 2, 1, 3)
    k_sb = rmsnorm(heads(k_sb), k_norm_g).astype(f32).transpose(0, 2, 1, 3)
    v_sb = heads(v_sb).astype(f32).transpose(0, 2, 1, 3)
    o_sb = stick_breaking_attention(q_sb, k_sb, v_sb)
    o_sb = o_sb.transpose(0, 2, 1, 3).reshape(bsz, seq, BRANCH_WIDTH).astype(x.dtype)

    xs = token_shift(p_rw, rwkv_shift_mu).astype(f32)
    c1 = BRANCH_WIDTH
    offs = [c1, 2 * c1, 3 * c1, 3 * c1 + DECAY_LORA, 3 * c1 + DECAY_LORA + ICLR_LORA]
    r, k, v, w_lo, a_lo, g_lo = jnp.split(xs, offs, axis=-1)
    w_log = -jax.nn.softplus(-(decay_base + jnp.tanh(w_lo) @ decay_up)) - 0.5
    decay = jnp.exp(-jnp.exp(w_log))
    iclr = jax.nn.sigmoid(iclr_base + a_lo @ iclr_up)
    gate = jax.nn.sigmoid(g_lo) @ out_gate_up
    kk = heads(k * key_k)
    kk = kk / jnp.maximum(jnp.linalg.norm(kk, axis=-1, keepdims=True), 1e-12)
    k = k * (1.0 + (iclr - 1.0) * key_a)
    rh, wh, kh, vh, ah = heads(r), heads(decay), heads(k), heads(v), heads(iclr)
    y = rwkv7_recurrence(rh, wh, kh, vh, -kk, kk * ah)
    mu = jnp.mean(y, axis=-1, keepdims=True)
    var = jnp.mean(jnp.square(y - mu), axis=-1, keepdims=True)
    y = ((y - mu) * lax.rsqrt(var + GN_EPS)).reshape(bsz, seq, BRANCH_WIDTH)
    y = y * group_norm_w + group_norm_b
    bonus = jnp.sum(rh * kh * bonus_rk, axis=-1, keepdims=True) * vh
    o_rw = ((y + bonus.reshape(bsz, seq, BRANCH_WIDTH)) * gate).astype(x.dtype)

    branches = jnp.stack([o_sb, o_rw], axis=2)
    up = jnp.einsum('bsnc,ncd->bsnd', branches, w_branch)
    gates = jax.nn.sigmoid(p_gate.reshape(bsz, seq, N_BRANCHES, D_MODEL) + branch_gate_b)
    mixed = jnp.sum(gates * up, axis=2)
    x = x + mixed @ w_out

    h2 = rmsnorm(x, ffn_norm_g)
    u = causal_depthwise_conv(h2 @ w_ffn_up, ffn_conv_w, ffn_conv_b)
    u_gate, u_val = jnp.split(u, 2, axis=-1)
    x = x + (jax.nn.silu(u_gate) * u_val) @ w_ffn_down
    return x


def setup_inputs(seed: int = 0) -> dict:
    key = jax.random.key(seed)
    ks = jax.random.split(key, 26)
    L = DEPTH

    def nrm(k, shape, scale):
        return scale * jax.random.normal(k, shape, jnp.float32)

    def gain(k, shape):
        return 1.0 + 0.02 * jax.random.normal(k, shape, jnp.float32)

    return {
        "x": jax.random.normal(ks[0], (BATCH, SEQ, D_MODEL), jnp.float32),
        "attn_norm_g": gain(ks[1], (L, D_MODEL)),
        "w_in": nrm(ks[2], (L, D_MODEL, IN_COLS), D_MODEL ** -0.5),
        "q_norm_g": gain(ks[3], (L, HEAD_DIM)),
        "k_norm_g": gain(ks[4], (L, HEAD_DIM)),
        "rwkv_shift_mu": jax.random.uniform(ks[5], (L, RW_COLS), jnp.float32),
        "decay_base": jax.random.uniform(ks[6], (L, BRANCH_WIDTH), jnp.float32, minval=-6.0, maxval=1.0),
        "decay_up": nrm(ks[7], (L, DECAY_LORA, BRANCH_WIDTH), 0.5 * DECAY_LORA ** -0.5),
        "iclr_base": jax.random.uniform(ks[8], (L, BRANCH_WIDTH), jnp.float32, minval=-1.0, maxval=1.0),
        "iclr_up": nrm(ks[9], (L, ICLR_LORA, BRANCH_WIDTH), 0.5 * ICLR_LORA ** -0.5),
        "out_gate_up": nrm(ks[10], (L, GATE_LORA, BRANCH_WIDTH), GATE_LORA ** -0.5),
        "key_k": 0.85 + 0.02 * jax.random.normal(ks[11], (L, BRANCH_WIDTH), jnp.float32),
        "key_a": gain(ks[12], (L, BRANCH_WIDTH)),
        "bonus_rk": nrm(ks[13], (L, RW_HEADS, HEAD_DIM), 0.1),
        "group_norm_w": gain(ks[14], (L, BRANCH_WIDTH)),
        "group_norm_b": nrm(ks[15], (L, BRANCH_WIDTH), 0.02),
        "branch_gate_b": nrm(ks[16], (L, N_BRANCHES, D_MODEL), 0.02),
        "w_branch": nrm(ks[17], (L, N_BRANCHES, BRANCH_WIDTH, D_MODEL), BRANCH_WIDTH ** -0.5),
        "w_out": nrm(ks[18], (L, D_MODEL, D_MODEL), D_MODEL ** -0.5),
        "ffn_norm_g": gain(ks[19], (L, D_MODEL)),
        "w_ffn_up": nrm(ks[20], (L, D_MODEL, 2 * D_FF), D_MODEL ** -0.5),
        "ffn_conv_w": nrm(ks[21], (L, CONV_WIDTH, 2 * D_FF), CONV_WIDTH ** -0.5),
        "ffn_conv_b": nrm(ks[22], (L, 2 * D_FF), 0.02),
        "w_ffn_down": nrm(ks[23], (L, D_FF, D_MODEL), D_FF ** -0.5),
    }


def reference(x, attn_norm_g, w_in, q_norm_g, k_norm_g, rwkv_shift_mu, decay_base, decay_up,
              iclr_base, iclr_up, out_gate_up, key_k, key_a, bonus_rk, group_norm_w,
              group_norm_b, branch_gate_b, w_branch, w_out, ffn_norm_g, w_ffn_up,
              ffn_conv_w, ffn_conv_b, w_ffn_down):
    for layer in range(DEPTH):
        x = hybrid_layer(
            x, attn_norm_g[layer], w_in[layer], q_norm_g[layer], k_norm_g[layer],
            rwkv_shift_mu[layer], decay_base[layer], decay_up[layer], iclr_base[layer],
            iclr_up[layer], out_gate_up[layer], key_k[layer], key_a[layer], bonus_rk[layer],
            group_norm_w[layer], group_norm_b[layer], branch_gate_b[layer], w_branch[layer],
            w_out[layer], ffn_norm_g[layer], w_ffn_up[layer], ffn_conv_w[layer],
            ffn_conv_b[layer], w_ffn_down[layer])
    return x
```

```python
import contextlib
import os
import numpy as np
import concourse.bass as bass
import concourse.mybir as mybir
from concourse.bass_utils import run_bass_kernel_spmd

F32 = mybir.dt.float32
BF16 = mybir.dt.bfloat16
AF = mybir.ActivationFunctionType
ALU = mybir.AluOpType

D = 1024
DFF = 2816
NFC = 22
IN_COLS = 5408
RW0 = 1536
G0 = 3360
EPS = 1e-6
GN_EPS = 64e-5
ALPHA = 0.6065306597126334
NEG = -30000.0

P_ATTN_G = 0
P_FFN_G = 8
P_QG = 16
P_KG = 17
P_MU = 18
P_DBASE = 33
P_IBASE = 37
P_KEYK = 41
P_KEYA = 45
P_GNW = 49
P_GNB = 53
P_BONUS = 57
P_BGATE = 61
P_CW0 = 77
P_CW1 = 121
P_CW2 = 165
P_CB = 209
P_QGS = 253
NPAR = 256

C_ID = 0
C_MASK = 128
C_ONESBD = 256
C_RWMASK = 384
C_ID64 = 640
NCST = 704

N_DMA_SEMS = 40


class Res:
    __slots__ = ("w", "r")

    def __init__(self):
        self.w = None
        self.r = []


class Tl:
    def __init__(self, h):
        self.h = h
        self.r = Res()

    def __getitem__(self, k):
        return self.h[k]


class Sched:
    ENGS = ("pe", "act", "dve", "pool", "sp")

    def __init__(self, nc, stack):
        self.nc = nc
        self.streams = {e: [] for e in self.ENGS}
        self.cnt = {e: 0 for e in self.ENGS}
        self.sems = {e: stack.enter_context(nc.semaphore("s_" + e)) for e in self.ENGS}
        self.dsems = [stack.enter_context(nc.semaphore("d%d" % i)) for i in range(N_DMA_SEMS)]
        self.dcnt = [0] * N_DMA_SEMS
        self.drr = 0
        self.drr_pool = 0
        self.waited = {e: {} for e in self.ENGS}
        self.nops = {e: 0 for e in self.ENGS}
        self.trace = []

    def _sem(self, key):
        return self.sems[key] if isinstance(key, str) else self.dsems[key]

    def _collect(self, eng, reads, writes):
        need = {}

        def add(ev, raw):
            if ev is None:
                return
            key, val, src = ev
            if isinstance(key, str) and src == eng:
                if eng == "pe":
                    return
            if self.waited[eng].get(key, 0) >= val:
                return
            if need.get(key, 0) < val:
                need[key] = val

        for r in reads:
            add(r.w, True)
        for w in writes:
            add(w.w, False)
            for e in w.r:
                add(e, False)
        for key, val in need.items():
            self.waited[eng][key] = val
        return list(need.items())

    def op(self, eng, fn, reads=(), writes=(), inc=True):
        waits = self._collect(eng, reads, writes)
        sem = self.sems[eng]
        val = self.cnt[eng] + 1
        if inc:
            self.cnt[eng] = val
        ev = (eng, val, eng)
        for r in reads:
            r.r.append(ev)
        for w in writes:
            w.w = ev
            w.r = []
        wl = [(self._sem(k), v) for k, v in waits]

        def run(e, fn=fn, wl=wl, sem=sem, inc=inc):
            for s, v in wl:
                e.wait_ge(s, v)
            ins = fn(e)
            if inc:
                ins.then_inc(sem, 1)

        self.streams[eng].append(run)
        self.nops[eng] += 1
        self.trace.append((eng, list(waits), (eng, 1) if inc else None))

    def dma(self, q, out, in_, reads=(), writes=()):
        if q == "pool":
            i = N_DMA_SEMS - 8 + self.drr_pool
            self.drr_pool = (self.drr_pool + 1) % 8
        else:
            i = self.drr
            self.drr = (self.drr + 1) % (N_DMA_SEMS - 8)
        prev = self.dcnt[i]
        waits = self._collect(q, reads, writes)
        if prev > 0 and self.waited[q].get(i, 0) < prev:
            self.waited[q][i] = prev
            waits.append((i, prev))
        val = prev + 16
        self.dcnt[i] = val
        ev = (i, val, q)
        for r in reads:
            r.r.append(ev)
        for w in writes:
            w.w = ev
            w.r = []
        wl = [(self._sem(k), v) for k, v in waits]
        dsem = self.dsems[i]

        def run(e, wl=wl, dsem=dsem, out=out, in_=in_):
            for s, v in wl:
                e.wait_ge(s, v)
            e.dma_start(out=out, in_=in_).then_inc(dsem, 16)

        self.streams[q].append(run)
        self.nops[q] += 1
        self.trace.append((q, list(waits), (i, 16)))

    def barrier(self):
        for e in self.ENGS:
            wl = []
            self._last_bar = []
            for o in self.ENGS:
                if o != e and o != "sp" and self.cnt[o] > self.waited[e].get(o, 0):
                    self.waited[e][o] = self.cnt[o]
                    wl.append((self.sems[o], self.cnt[o]))
                    self._last_bar.append((o, self.cnt[o]))
            for i in range(N_DMA_SEMS):
                if self.dcnt[i] > self.waited[e].get(i, 0):
                    self.waited[e][i] = self.dcnt[i]
                    wl.append((self.dsems[i], self.dcnt[i]))
                    self._last_bar.append((i, self.dcnt[i]))
            if wl:
                def run(eng, wl=wl):
                    for s, v in wl:
                        eng.wait_ge(s, v)
                self.streams[e].append(run)
                self.trace.append((e, [(k, v) for k, v in self._last_bar], None))

    def simulate(self):
        per = {e: [t for t in self.trace if t[0] == e] for e in self.ENGS}
        pc = {e: 0 for e in self.ENGS}
        val = {}
        progress = True
        while progress:
            progress = False
            for e in self.ENGS:
                while pc[e] < len(per[e]):
                    _, waits, inc = per[e][pc[e]]
                    if all(val.get(k, 0) >= v for k, v in waits):
                        if inc is not None:
                            val[inc[0]] = val.get(inc[0], 0) + inc[1]
                        pc[e] += 1
                        progress = True
                    else:
                        break
        stuck = {e: (pc[e], len(per[e])) for e in self.ENGS if pc[e] < len(per[e])}
        return stuck, val

    def emit(self):
        nc = self.nc
        with nc.Block() as block:
            @block.tensor
            def _(e):
                for f in self.streams["pe"]:
                    f(e)

            @block.scalar
            def _(e):
                for f in self.streams["act"]:
                    f(e)

            @block.vector
            def _(e):
                for f in self.streams["dve"]:
                    f(e)

            @block.gpsimd
            def _(e):
                for f in self.streams["pool"]:
                    f(e)

            @block.sync
            def _(e):
                for f in self.streams["sp"]:
                    f(e)


class KB:
    def __init__(self, T, NSEQ, dbg=False):
        self.T = T
        self.NSEQ = NSEQ
        self.dbg = dbg
        self.NT = T // 128
        self.NTB = T // 512

    def alloc(self, name, shape, dt, persist=False):
        nb = 4 if dt == F32 else 2
        n = nb
        for s in shape[1:]:
            n *= s
        n = (n + 31) // 32 * 32
        if persist:
            off = self.p_off
            self.p_off += n
            assert self.s_off == self.p_off or self.s_base is None
        else:
            off = self.s_off
            self.s_off += n
        assert off + n <= 229000, (name, off, n)
        self.uid += 1
        h = self.nc.alloc_sbuf_tensor_at("%s_%d" % (name, self.uid), list(shape), dt, offset=off)
        return Tl(h)

    def phase(self):
        self.S.barrier()
        self.s_off = self.s_base
        self.ps_rr = 0
        self.psb_rr = 0

    def act(self, out, in_, func, reads, writes, bias=0.0, scale=1.0, accum=None):
        if accum is None:
            self.S.op("act", lambda e: e.activation(out=out, in_=in_, func=func, bias=bias, scale=scale), reads, writes)
        else:
            self.S.op("act", lambda e: e.activation(out=out, in_=in_, func=func, bias=bias, scale=scale, accum_out=accum), reads, writes)

    def tt(self, eng, out, in0, in1, op, reads, writes):
        self.S.op(eng, lambda e: e.tensor_tensor(out=out, in0=in0, in1=in1, op=op), reads, writes)

    def ts(self, eng, out, in0, s1, op0, reads, writes, s2=None, op1=None):
        if op1 is None:
            self.S.op(eng, lambda e: e.tensor_scalar(out=out, in0=in0, scalar1=s1, scalar2=None, op0=op0), reads, writes)
        else:
            self.S.op(eng, lambda e: e.tensor_scalar(out=out, in0=in0, scalar1=s1, scalar2=s2, op0=op0, op1=op1), reads, writes)

    def stt(self, out, in0, scalar, in1, op0, op1, reads, writes):
        self.S.op("dve", lambda e: e.scalar_tensor_tensor(out=out, in0=in0, scalar=scalar, in1=in1, op0=op0, op1=op1), reads, writes)

    def cp(self, eng, out, in_, reads, writes):
        if eng == "act":
            self.act(out, in_, AF.Copy, reads, writes)
        else:
            self.S.op(eng, lambda e: e.tensor_copy(out=out, in_=in_), reads, writes)

    def mm(self, out, lhsT, rhs, reads, writes, start=True, stop=True, inc=True):
        self.S.op("pe", lambda e: e.matmul(out, lhsT=lhsT, rhs=rhs, start=start, stop=stop), reads, writes, inc=inc)

    def tr(self, out, in_, ident, reads, writes, inc=True):
        self.S.op("pe", lambda e: e.transpose(out=out, in_=in_, identity=ident), reads, writes, inc=inc)

    def ps_half(self):
        p, r = self.ps_full()
        return p[:, 0:256], r

    def ps_full(self):
        i = self.ps_rr
        self.ps_rr = (self.ps_rr + 1) % 6
        return self.psA[:, i * 512:(i + 1) * 512], [self.psR[i]]

    def ps_bf(self):
        i = self.psb_rr
        self.psb_rr = (self.psb_rr + 1) % 2
        return self.psB[:, i * 1024:i * 1024 + 512], [self.psbR[i]]

    def par(self, col, n=128):
        return self.params[0:n, col:col + 1]

    def load_w(self, src, KC, ncols, np_=128, dst=None):
        st = self.wst[self.wst_rr % len(self.wst)]
        self.wst_rr += 1
        if dst is None:
            wb = self.wbf[self.wbf_rr % len(self.wbf)]
            self.wbf_rr += 1
        else:
            wb = dst
        self.S.dma("sp", st[0:np_, 0:KC, 0:ncols], src, writes=[st.r])
        self.cp("pool", wb[0:np_, 0:KC, 0:ncols], st[0:np_, 0:KC, 0:ncols], [st.r], [wb.r])
        return wb

    def w_in_chunk(self, col0, ncols):
        return self.w_in.rearrange("(kc p) n -> p kc n", p=128)[:, :, col0:col0 + ncols]

    def build(self):
        T, NSEQ = self.T, self.NSEQ
        nc = bass.Bass("TRN2", target_bir_lowering=False, disable_frame_to_traceback=True)
        self.nc = nc
        dr = lambda n, shp, kind="ExternalInput": nc.dram_tensor(n, shp, F32, kind=kind).ap()
        self.x = dr("x", [NSEQ, T, D])
        self.w_in = dr("w_in", [D, IN_COLS])
        self.w_br = dr("w_branch", [2, 512, D])
        self.w_out = dr("w_out", [D, D])
        self.w_up = dr("w_ffn_up", [D, 2 * DFF])
        self.w_dn = dr("w_ffn_down", [DFF, D])
        self.d_up = dr("decay_up", [64, 512])
        self.i_up = dr("iclr_up", [64, 512])
        self.g_up = dr("out_gate_up", [160, 512])
        self.par_d = dr("params", [128, NPAR])
        self.cst_d = dr("consts", [128, NCST])
        self.gbc = dr("gbc", [2, 128, D])
        self.out = dr("out", [NSEQ, T, D], kind="ExternalOutput")
        if self.dbg:
            self.dbg_o = dr("dbg_o", [NSEQ, 8, 128, T], kind="ExternalOutput")
            self.dbg_m = dr("dbg_m", [NSEQ, 8, 128, T], kind="ExternalOutput")

        with contextlib.ExitStack() as st:
            self.S = Sched(nc, st)
            S = self.S
            self.uid = 0
            self.p_off = 16384
            self.s_off = 16384
            self.s_base = None
            self.psA_t = st.enter_context(nc.psum_tensor("psA", [128, 6 * 512], F32))
            self.psB_t = st.enter_context(nc.psum_tensor("psB", [128, 2 * 1024], BF16))
            self.psA = self.psA_t
            self.psB = self.psB_t
            self.psR = [Res() for _ in range(6)]
            self.psbR = [Res() for _ in range(2)]
            self.ps_rr = 0
            self.psb_rr = 0

            P = lambda n, shp, dt: self.alloc(n, shp, dt, persist=True)
            self.cst = P("cst", [128, NCST], F32)
            self.params = P("params", [128, NPAR], F32)
            self.identb = P("identb", [128, 128], BF16)
            self.maskb = P("maskb", [128, 128], BF16)
            self.onesbd = P("onesbd", [128, 128], BF16)
            self.ones = P("ones", [128, 2048], F32)
            self.reset = P("reset", [128, 512], F32)
            self.luw = P("luw", [128, 512], BF16)
            self.gu1 = P("gu1", [128, 512], BF16)
            self.gu2 = P("gu2", [32, 512], BF16)
            self.wout = P("wout", [128, 8, 1024], BF16)
            self.hT = P("hT", [128, 8, T], BF16)
            big_off = self.p_off
            self.BIG = P("BIG", [128, 16, T], BF16)
            self.hTr = [Res() for _ in range(8)]
            self.bigR = [Res() for _ in range(16)]
            self.s_base = self.p_off
            self.s_off = self.s_base
            self.identf = self.cst[:, C_ID:C_ID + 128]
            self.rwmask = self.cst[:, C_RWMASK:C_RWMASK + 256]
            self.id64 = self.cst[:, C_ID64:C_ID64 + 64]
            self.big_off = big_off

            import os as _os
            ph = _os.environ.get("KB_PHASES", "A,B,C,D,E,F,G").split(",")
            self.setup()
            for s in range(NSEQ):
                if "A" in ph:
                    self.phaseA(s)
                if "B" in ph:
                    self.phaseB(s)
                if "C" in ph:
                    self.phaseC(s)
                if "D" in ph:
                    self.phaseD(s)
                if "E" in ph:
                    self.phaseE(s)
                for hf in range(2):
                    if "F" in ph:
                        self.phaseF(s, hf)
                    if "G" in ph:
                        self.phaseG(s, hf)
            S.barrier()
            stuck, val = S.simulate()
            print('SCHED ops', S.nops, 'cnt', S.cnt, 'stuck', stuck, flush=True)
            assert not stuck, stuck
            S.emit()
        return nc

    def setup(self):
        S = self.S
        self.wst = [self.alloc("wst%d" % i, [128, 8, 128], F32) for i in range(2)]
        self.wbf = [self.alloc("wbf%d" % i, [128, 8, 128], BF16) for i in range(4)]
        self.wst_rr = 0
        self.wbf_rr = 0
        stg = self.alloc("stg", [128, 512], F32)
        stg2 = self.alloc("stg2", [128, 512], F32)
        stg3 = self.alloc("stg3", [32, 512], F32)
        S.dma("sp", self.cst[:], self.cst_d[:, :], writes=[self.cst.r])
        S.dma("sp", self.params[:], self.par_d[:, :], writes=[self.params.r])
        S.dma("sp", stg[0:64, :], self.d_up[:, :], writes=[stg.r])
        S.dma("sp", stg[64:128, :], self.i_up[:, :], writes=[stg.r])
        S.dma("sp", stg2[:], self.g_up[0:128, :], writes=[stg2.r])
        S.dma("sp", stg3[:], self.g_up[128:160, :], writes=[stg3.r])
        self.cp("dve", self.identb[:], self.cst[:, C_ID:C_ID + 128], [self.cst.r], [self.identb.r])
        self.cp("dve", self.maskb[:], self.cst[:, C_MASK:C_MASK + 128], [self.cst.r], [self.maskb.r])
        self.cp("dve", self.onesbd[:], self.cst[:, C_ONESBD:C_ONESBD + 128], [self.cst.r], [self.onesbd.r])
        self.cp("dve", self.luw[:], stg[:], [stg.r], [self.luw.r])
        self.cp("dve", self.gu1[:], stg2[:], [stg2.r], [self.gu1.r])
        self.cp("dve", self.gu2[:], stg3[:], [stg3.r], [self.gu2.r])
        S.op("pool", lambda e: e.memset(self.ones[:], 1.0), [], [self.ones.r])
        S.op("pool", lambda e: e.memset(self.reset[:], 1.0), [], [self.reset.r])
        S.op("pool", lambda e: e.memset(self.reset[:].rearrange("p (c j) -> p c j", j=64)[:, :, 0:1], 0.0), [], [self.reset.r])
        self.ts("dve", self.params[:, P_QGS:P_QGS + 1], self.params[:, P_QG:P_QG + 1], 0.125, ALU.mult,
                [self.params.r], [self.params.r])
        for c in range(8):
            src = self.w_out.rearrange("(kc p) n -> p kc n", p=128)[:, :, c * 128:(c + 1) * 128]
            wb = self.load_w(src, 8, 128)
            self.cp("dve", self.wout[:, :, c * 128:(c + 1) * 128], wb[:, :, :], [wb.r], [self.wout.r])

    def alloc_w(self, nbf=4):
        self.wst = [self.alloc("wst%d" % i, [128, 8, 128], F32) for i in range(2)]
        self.wbf = [self.alloc("wbf%d" % i, [128, 8, 128], BF16) for i in range(nbf)]
        self.wst_rr = 0
        self.wbf_rr = 0

    def norm_transpose(self, xt, xn, xnb, ss, rs, gB, tok0):
        self.act(xn[:], xt[:], AF.Square, [xt.r], [xn.r, ss.r], accum=ss[:])
        self.act(rs[:], ss[:], AF.Ln, [ss.r], [rs.r], scale=1.0 / D, bias=EPS)
        self.act(rs[:], rs[:], AF.Exp, [rs.r], [rs.r], scale=-0.5)
        self.stt(xnb[:], xt[:], rs[:], gB[:], ALU.mult, ALU.mult, [xt.r, rs.r, gB.r], [xnb.r])
        for half in range(2):
            tp, tpr = self.ps_bf()
            for c4 in range(4):
                c = half * 4 + c4
                self.tr(tp[:, c4 * 128:(c4 + 1) * 128], xnb[:, c * 128:(c + 1) * 128], self.identb[:],
                        [xnb.r, self.identb.r], tpr, inc=(c4 == 3))
            dst = self.hT[:, half * 4:(half + 1) * 4, tok0:tok0 + 128]
            src = tp.rearrange("p (c t) -> p c t", t=128)
            self.cp("dve" if half == 0 else "act", dst, src, tpr, [self.hTr[half * 4 + i] for i in range(4)])

    def phaseA(self, s):
        self.phase()
        S = self.S
        xin = [self.alloc("xin", [128, D], F32) for _ in range(2)]
        xn = [self.alloc("xn", [128, D], F32) for _ in range(1)]
        xnb = [self.alloc("xnb", [128, D], BF16) for _ in range(2)]
        ss = [self.alloc("ss", [128, 1], F32) for _ in range(2)]
        rs = [self.alloc("rs", [128, 1], F32) for _ in range(2)]
        gB = self.alloc("gB", [128, D], F32)
        S.dma("sp", gB[:], self.gbc[0, :, :], writes=[gB.r])
        for i in range(self.NT):
            b = i % 2
            S.dma("sp", xin[b][:], self.x[s, i * 128:(i + 1) * 128, :], writes=[xin[b].r])
            self.norm_transpose(xin[b], xn[0], xnb[b], ss[b], rs[b], gB, i * 128)

    def proj(self, wb, KC, rhs_fn, rhs_res, ntb, evac, m=128, base=0):
        for tb in range(ntb):
            ps, pr = self.ps_full()
            for kc in range(KC):
                self.mm(ps[0:m, :], wb[:, kc, 0:m], rhs_fn(kc, tb), [wb.r] + rhs_res(kc), pr,
                        start=(kc == 0), stop=(kc == KC - 1), inc=(kc == KC - 1))
            evac(tb, ps, pr)

    def phaseB(self, s):
        T, NT, NTB = self.T, self.NT, self.NTB
        S = self.S
        self.phase()
        self.alloc_w()
        BIG = self.BIG
        q32 = [self.alloc("q32", [128, 512], F32) for _ in range(2)]
        sq = [self.alloc("sq", [128, 512], BF16) for _ in range(2)]
        rt = [self.alloc("rt", [128, 512], F32) for _ in range(2)]
        cnt = [0]
        for c in range(8):
            wb = self.load_w(self.w_in_chunk(c * 128, 128), 8, 128)
            gcol = self.par(P_QGS) if c < 4 else self.par(P_KG)

            def evac(tb, ps, pr, c=c, gcol=gcol):
                b = cnt[0] % 2
                cnt[0] += 1
                self.act(q32[b][:], ps, AF.Copy, pr, [q32[b].r])
                self.act(sq[b][:], ps, AF.Square, pr, [sq[b].r])
                ps2, pr2 = self.ps_full()
                self.mm(ps2, self.onesbd[:], sq[b][:], [self.onesbd.r, sq[b].r], pr2)
                self.act(rt[b][:], ps2, AF.Ln, pr2, [rt[b].r], scale=1.0 / 64, bias=EPS)
                self.act(rt[b][:], rt[b][:], AF.Exp, [rt[b].r], [rt[b].r], scale=-0.5)
                self.stt(BIG[:, 8 + c, tb * 512:(tb + 1) * 512], q32[b][:], gcol, rt[b][:], ALU.mult, ALU.mult,
                         [q32[b].r, rt[b].r, self.params.r], [self.bigR[8 + c]])

            self.proj(wb, 8, lambda kc, tb: self.hT[:, kc, tb * 512:(tb + 1) * 512], lambda kc: [self.hTr[kc]], NTB, evac)

        self.uid += 1
        vsb = Tl(self.nc.alloc_sbuf_tensor_at("vsb_%d" % self.uid, [128, NT, 512], BF16, offset=self.big_off + 4 * T * 2))
        wv = self.alloc("wv", [128, 8, 512], BF16)
        for j in range(4):
            wb = self.load_w(self.w_in_chunk(1024 + j * 128, 128), 8, 128)
            self.cp("dve", wv[:, :, j * 128:(j + 1) * 128], wb[:, :, :], [wb.r], [wv.r])
        for i in range(NT):
            ps, pr = self.ps_full()
            for kc in range(8):
                self.mm(ps, self.hT[:, kc, i * 128:(i + 1) * 128], wv[:, kc, :], [self.hTr[kc], wv.r], pr,
                        start=(kc == 0), stop=(kc == 7), inc=(kc == 7))
            self.cp("act" if i % 2 else "dve", vsb[:, i, :], ps, pr, [vsb.r])

        self.phase()
        ebuf = [self.alloc("ebuf", [128, 2048], F32) for _ in range(2)]
        cpb = [self.alloc("cpb", [128, 2049], F32) for _ in range(2)]
        wbuf = [self.alloc("wbuf", [128, 2048], BF16) for _ in range(2)]
        wT = [self.alloc("wT", [128, 16, 128], BF16) for _ in range(2)]
        ngt = [self.alloc("ngt", [128, 1], F32) for _ in range(2)]
        for b in range(2):
            S.op("pool", lambda e, b=b: e.memset(cpb[b][:, 0:1], 0.0), [], [cpb[b].r])
        zp = self.psA[:, 0:2048]
        zr = self.psR[0:4]
        it = 0
        for hp in range(4):
            for qb in range(NT):
                ovp = self.psA[:, 2048 + (qb % 2) * 512: 2048 + (qb % 2) * 512 + 128]
                ovr = [self.psR[4 + (qb % 2)]]
                for hh in range(2):
                    sl = slice(hh * 64, hh * 64 + 64)
                    head = 2 * hp + hh
                    b = it % 2
                    it += 1
                    nk = (qb + 1) * 128
                    nd = qb * 128
                    qT = BIG[sl, 8 + hp, qb * 128:(qb + 1) * 128]
                    rd = [self.bigR[8 + hp], self.bigR[12 + hp]]
                    c0 = 0
                    while c0 < nd:
                        c1 = min(nd, c0 + 512)
                        self.mm(zp[:, c0:c1], qT, BIG[sl, 12 + hp, c0:c1], rd, zr, inc=False)
                        c0 = c1
                    self.mm(zp[:, nd:nk], qT, BIG[sl, 12 + hp, nd:nk], rd, zr, start=True, stop=False, inc=False)
                    self.mm(zp[:, nd:nk], self.identb[:], self.maskb[:], [self.identb.r, self.maskb.r], zr,
                            start=False, stop=True)
                    E, C, W, WT, NG = ebuf[b], cpb[b], wbuf[b], wT[b], ngt[b]
                    self.act(E[:, 0:nk], zp[:, 0:nk], AF.Exp, zr, [E.r])
                    self.act(E[:, 0:nk], E[:, 0:nk], AF.Ln, [E.r], [E.r], bias=1.0)
                    S.op("dve", lambda e, C=C, E=E, nk=nk: e.tensor_tensor_scan(
                        out=C[:, 1:nk + 1], data0=self.ones[:, 0:nk], data1=E[:, 0:nk], initial=0.0,
                        op0=ALU.mult, op1=ALU.add), [self.ones.r, E.r], [C.r])
                    self.ts("pool", NG[:], C[:, nk:nk + 1], -1.0, ALU.mult, [C.r], [NG.r])
                    self.tt("dve", E[:, 0:nk], zp[:, 0:nk], C[:, 0:nk], ALU.add, zr + [C.r], [E.r])
                    self.act(W[:, 0:nk], E[:, 0:nk], AF.Exp, [E.r, NG.r], [W.r], bias=NG[:])
                    for k0 in range(0, qb + 1, 4):
                        k1 = min(qb + 1, k0 + 4)
                        tp, tpr = self.ps_bf()
                        for kb in range(k0, k1):
                            self.tr(tp[:, (kb - k0) * 128:(kb - k0 + 1) * 128], W[:, kb * 128:(kb + 1) * 128], self.identb[:],
                                    [W.r, self.identb.r], tpr, inc=(kb == k1 - 1))
                        n_ = (k1 - k0) * 128
                        eng = "act" if (k0 // 4) % 2 else "dve"
                        self.cp(eng, WT[:, k0:k1, :], tp[:, 0:n_].rearrange("p (k q) -> p k q", q=128), tpr, [WT.r])
                    for kb in range(qb + 1):
                        self.mm(ovp[sl, :], vsb[:, kb, head * 64:(head + 1) * 64], WT[:, kb, :], [vsb.r, WT.r], ovr,
                                start=(kb == 0), stop=(kb == qb), inc=(kb == qb))
                self.cp("act", BIG[:, hp, qb * 128:(qb + 1) * 128], ovp, ovr, [self.bigR[hp]])

    def phaseC(self, s):
        T, NT, NTB = self.T, self.NT, self.NTB
        S = self.S
        self.phase()
        self.alloc_w(0)
        BIG = self.BIG
        A = lambda n, shp=(128, 512), dt=F32: self.alloc(n, list(shp), dt)
        r32, k32, v32 = A("r32"), A("k32"), A("v32")
        sg, cum, g, gp, gi, gc = A("sg"), A("cum"), A("g"), A("gp"), A("gi"), A("gc")
        ic, gate, t1, t2, kk, kb_, kp = A("ic"), A("gate"), A("t1"), A("t2"), A("kk"), A("kb"), A("kp")
        bon, YT = A("bon"), A("YT")
        yc = kk
        sqb, rkb, yb = A("sqb", dt=BF16), A("rkb", dt=BF16), A("yb", dt=BF16)
        AR = self.alloc("AR", [128, 8, 128], BF16)
        BK = self.alloc("BK", [128, 8, 128], BF16)
        BG, KG, VB = A("BG", dt=BF16), A("KG", dt=BF16), A("VB", dt=BF16)
        pbuf = [A("pbuf", (128, 513)) for _ in range(2)]
        carry = [A("carry", (128, 1)) for _ in range(3)]
        pcnt = [0]

        def shift_evac(ps, pr, tb, mucol, dst, cy, m=128):
            cur = pbuf[pcnt[0] % 2]
            pcnt[0] += 1
            if tb == 0:
                S.op("pool", lambda e, cur=cur: e.memset(cur[:, 0:1], 0.0), [], [cur.r])
            else:
                self.cp("pool", cur[0:m, 0:1], cy[0:m, 0:1], [cy.r], [cur.r])
            self.act(cur[0:m, 1:513], ps[0:m, :], AF.Copy, pr, [cur.r])
            self.cp("pool", cy[0:m, 0:1], cur[0:m, 512:513], [cur.r], [cy.r])
            self.tt("pool", t1[0:m, :], cur[0:m, 0:512], cur[0:m, 1:513], ALU.subtract, [cur.r], [t1.r])
            self.stt(dst[0:m, :], t1[0:m, :], mucol, cur[0:m, 1:513], ALU.mult, ALU.add,
                     [t1.r, cur.r, self.params.r], [dst.r])

        wr3 = [self.alloc("wr3", [128, 8, 128], BF16) for _ in range(3)]
        hTf = lambda kc, tb: self.hT[:, kc, tb * 512:(tb + 1) * 512]
        hTr = lambda kc: [self.hTr[kc]]
        xs = t2
        wb = self.load_w(self.w_in_chunk(RW0 + 12 * 128, 128), 8, 128, dst=wr3[0])

        def ev_wl(tb, ps, pr):
            shift_evac(ps, pr, tb, self.par(P_MU + 12), xs, carry[0])
            self.act(BIG[0:64, 8, tb * 512:(tb + 1) * 512], xs[0:64, :], AF.Tanh, [xs.r], [self.bigR[8]])
            self.cp("pool", BIG[64:128, 8, tb * 512:(tb + 1) * 512], xs[64:128, :], [xs.r], [self.bigR[8]])
        self.proj(wb, 8, hTf, hTr, NTB, ev_wl)
        wb = self.load_w(self.w_in_chunk(RW0 + 13 * 128, 128), 8, 128, dst=wr3[1])

        def ev_g1(tb, ps, pr):
            shift_evac(ps, pr, tb, self.par(P_MU + 13), xs, carry[1])
            self.act(BIG[:, 9, tb * 512:(tb + 1) * 512], xs[:, :], AF.Sigmoid, [xs.r], [self.bigR[9]])
        self.proj(wb, 8, hTf, hTr, NTB, ev_g1)
        wb = self.load_w(self.w_in_chunk(RW0 + 14 * 128, 32), 8, 32, dst=wr3[2])

        def ev_g2(tb, ps, pr):
            shift_evac(ps, pr, tb, self.par(P_MU + 14, 32), xs, carry[2], m=32)
            self.act(BIG[0:32, 10, tb * 512:(tb + 1) * 512], xs[0:32, :], AF.Sigmoid, [xs.r], [self.bigR[10]])
        self.proj(wb, 8, hTf, hTr, NTB, ev_g2, m=32)

        cs = []
        for i in range(2):
            d = dict(TR=A("TR", (128, 256), BF16), ABK=A("ABK", (128, 256), BF16),
                     L=[A("L", (128, 2, 128), BF16) for _ in range(5)], L5=A("L5", (128, 128), BF16),
                     X=A("X", (128, 128), BF16), RpT=A("RpT", (128, 64), BF16), MT=A("MT", (128, 64), BF16),
                     Hadd=A("Hadd", (128, 64), F32))
            S.op("pool", lambda e, d=d: e.memset(d["L"][0][:], 0.0), [], [d["L"][0].r])
            cs.append(d)
        Hst = [A("Hst", (128, 64), BF16) for _ in range(2)]
        idb = self.identb
        step = [0]

        for hp in range(4):
            for j in range(3):
                self.load_w(self.w_in_chunk(RW0 + (j * 4 + hp) * 128, 128), 8, 128, dst=wr3[j])
            S.op("pool", lambda e: e.memset(Hst[0][:], 0.0), [], [Hst[0].r])
            hcur = 0
            for tb in range(NTB):
                tsl = slice(tb * 512, (tb + 1) * 512)
                for j, dst in enumerate((r32, k32, v32)):
                    ps, pr = self.ps_full()
                    for kc in range(8):
                        self.mm(ps, wr3[j][:, kc, :], self.hT[:, kc, tsl], [wr3[j].r, self.hTr[kc]], pr,
                                start=(kc == 0), stop=(kc == 7), inc=(kc == 7))
                    shift_evac(ps, pr, tb, self.par(P_MU + j * 4 + hp), dst, carry[j])
                ps, pr = self.ps_full()
                self.mm(ps, self.luw[0:64, hp * 128:(hp + 1) * 128], BIG[0:64, 8, tsl], [self.luw.r, self.bigR[8]], pr)
                self.act(sg[:], ps, AF.Sigmoid, pr + [self.params.r], [sg.r], bias=self.par(P_DBASE + hp))
                ps, pr = self.ps_full()
                self.mm(ps, self.luw[64:128, hp * 128:(hp + 1) * 128], BIG[64:128, 8, tsl], [self.luw.r, self.bigR[8]], pr)
                self.act(ic[:], ps, AF.Sigmoid, pr + [self.params.r], [ic.r], bias=self.par(P_IBASE + hp))
                ps, pr = self.ps_full()
                self.mm(ps, self.gu1[:, hp * 128:(hp + 1) * 128], BIG[:, 9, tsl], [self.gu1.r, self.bigR[9]], pr,
                        start=True, stop=False, inc=False)
                self.mm(ps, self.gu2[0:32, hp * 128:(hp + 1) * 128], BIG[0:32, 10, tsl], [self.gu2.r, self.bigR[10]], pr,
                        start=False, stop=True)
                self.act(gate[:], ps, AF.Copy, pr, [gate.r])
                S.op("dve", lambda e: e.tensor_tensor_scan(out=cum[:], data0=self.reset[:], data1=sg[:], initial=0.0,
                                                           op0=ALU.mult, op1=ALU.add), [self.reset.r, sg.r], [cum.r])
                self.act(g[:], cum[:], AF.Exp, [cum.r], [g.r], scale=-ALPHA)
                self.act(gi[:], cum[:], AF.Exp, [cum.r], [gi.r], scale=ALPHA)
                self.tt("pool", t1[:], cum[:], sg[:], ALU.subtract, [cum.r, sg.r], [t1.r])
                self.act(gp[:], t1[:], AF.Exp, [t1.r], [gp.r], scale=-ALPHA)
                c3 = cum[:].rearrange("p (c j) -> p c j", j=64)
                self.tt("pool", t2[:].rearrange("p (c j) -> p c j", j=64), c3[:, :, 63:64].to_broadcast([128, 8, 64]), c3,
                        ALU.subtract, [cum.r], [t2.r])
                self.act(gc[:], t2[:], AF.Exp, [t2.r], [gc.r], scale=-ALPHA)
                self.act(sqb[:], k32[:], AF.Square, [k32.r, self.params.r], [sqb.r], scale=self.par(P_KEYK + hp))
                ps, pr = self.ps_full()
                self.mm(ps, self.onesbd[:], sqb[:], [self.onesbd.r, sqb.r], pr)
                self.ts("dve", t1[:], ps, 1e-18, ALU.max, pr, [t1.r])
                self.act(t1[:], t1[:], AF.Ln, [t1.r], [t1.r])
                self.act(t1[:], t1[:], AF.Exp, [t1.r], [t1.r], scale=-0.5)
                self.stt(kk[:], k32[:], self.par(P_KEYK + hp), t1[:], ALU.mult, ALU.mult, [k32.r, t1.r, self.params.r], [kk.r])
                self.tt("pool", kb_[:], kk[:], ic[:], ALU.mult, [kk.r, ic.r], [kb_.r])
                self.ts("dve", t2[:], ic[:], -1.0, ALU.add, [ic.r, self.params.r], [t2.r], s2=self.par(P_KEYA + hp), op1=ALU.mult)
                self.stt(kp[:], t2[:], 1.0, k32[:], ALU.add, ALU.mult, [t2.r, k32.r], [kp.r])
                AR4 = AR[:]
                BK4 = BK[:]
                v3 = lambda t: t[:].rearrange("p (c j) -> p c j", j=64)
                self.stt(AR4[:, :, 0:64], v3(kk), -1.0, v3(gp), ALU.mult, ALU.mult, [kk.r, gp.r], [AR.r])
                self.tt("pool", AR4[:, :, 64:128], v3(r32), v3(g), ALU.mult, [r32.r, g.r], [AR.r])
                self.tt("dve", BK4[:, :, 0:64], v3(kb_), v3(gi), ALU.mult, [kb_.r, gi.r], [BK.r])
                self.tt("pool", BK4[:, :, 64:128], v3(kp), v3(gi), ALU.mult, [kp.r, gi.r], [BK.r])
                self.tt("dve", BG[:], kb_[:], gc[:], ALU.mult, [kb_.r, gc.r], [BG.r])
                self.tt("pool", KG[:], kp[:], gc[:], ALU.mult, [kp.r, gc.r], [KG.r])
                self.cp("pool", VB[:], v32[:], [v32.r], [VB.r])
                self.stt(rkb[:], r32[:], self.par(P_BONUS + hp), kp[:], ALU.mult, ALU.mult, [r32.r, kp.r, self.params.r], [rkb.r])
                ps, pr = self.ps_full()
                self.mm(ps, self.onesbd[:], rkb[:], [self.onesbd.r, rkb.r], pr)
                self.tt("dve", bon[:], ps, v32[:], ALU.mult, pr + [v32.r], [bon.r])

                for n in range(8):
                    d = cs[step[0] % 2]
                    step[0] += 1
                    TR, ABK, L, L5, X, RpT, MT, Hadd = d["TR"], d["ABK"], d["L"], d["L5"], d["X"], d["RpT"], d["MT"], d["Hadd"]
                    csl = slice(n * 64, (n + 1) * 64)
                    tp, tpr = self.ps_bf()
                    for hh in range(2):
                        sl = slice(hh * 64, hh * 64 + 64)
                        idn = idb[sl, sl]
                        self.tr(tp[sl, 0:64], AR[sl, n, 0:64], idn, [AR.r, idb.r], tpr, inc=False)
                        self.tr(tp[sl, 64:128], VB[sl, csl], idn, [VB.r, idb.r], tpr, inc=False)
                        self.tr(tp[sl, 128:192], BG[sl, csl], idn, [BG.r, idb.r], tpr, inc=False)
                        self.tr(tp[sl, 192:256], KG[sl, csl], idn, [KG.r, idb.r], tpr, inc=(hh == 1))
                    self.cp("act", TR[:], tp[:, 0:256], tpr, [TR.r])
                    self.cp("dve", X[:, 0:64], tp[:, 0:64], tpr, [X.r])
                    ps, pr = self.ps_half()
                    for hh in range(2):
                        sl = slice(hh * 64, hh * 64 + 64)
                        self.mm(ps[sl, 0:128], BK[sl, n, 0:64], AR[sl, n, :], [BK.r, AR.r], pr, inc=False)
                        self.mm(ps[sl, 128:256], BK[sl, n, 64:128], AR[sl, n, :], [BK.r, AR.r], pr, inc=(hh == 1))
                    self.tt("dve", ABK[:], ps, self.rwmask, ALU.mult, pr + [self.cst.r], [ABK.r])
                    L0 = L[0]
                    self.cp("pool", L0[0:64, 1, 0:64], ABK[0:64, 0:64], [ABK.r], [L0.r])
                    self.cp("pool", L0[64:128, 1, 64:128], ABK[64:128, 0:64], [ABK.r], [L0.r])
                    tp2, tpr2 = self.ps_bf()
                    self.tr(tp2[:, 0:128], L0[:, 1, :], idb[:], [L0.r, idb.r], tpr2)
                    self.cp("act", L0[:, 0, :], tp2[:, 0:128], tpr2, [L0.r])
                    for k in range(1, 5):
                        ps, pr = self.ps_half()
                        Lp = L[k - 1]
                        self.mm(ps[:, 0:128], Lp[:, 1, :], Lp[:, 0, :], [Lp.r], pr, inc=False)
                        self.mm(ps[:, 128:256], Lp[:, 0, :], Lp[:, 1, :], [Lp.r], pr)
                        self.cp("act" if k % 2 else "dve", L[k][:].rearrange("p a b -> p (a b)"), ps, pr, [L[k].r])
                    ps, pr = self.ps_half()
                    self.mm(ps[:, 0:128], L[4][:, 0, :], L[4][:, 1, :], [L[4].r], pr)
                    self.cp("act", L5[:], ps[:, 0:128], pr, [L5.r])
                    ps, pr = self.ps_half()
                    for hh in range(2):
                        sl = slice(hh * 64, hh * 64 + 64)
                        self.mm(ps[sl, 0:64], ABK[sl, 128:192], TR[sl, 64:128], [ABK.r, TR.r], pr, inc=(hh == 1))
                    self.cp("dve", X[:, 64:128], ps[:, 0:64], pr, [X.r])
                    for k in range(6):
                        PT = L[k][:, 1, :] if k < 5 else L5[:]
                        PTr = L[k].r if k < 5 else L5.r
                        ps, pr = self.ps_half()
                        self.mm(ps[:, 0:128], PT, X[:], [PTr, X.r], pr)
                        self.tt("dve", X[:], ps[:, 0:128], X[:], ALU.add, pr + [X.r], [X.r])
                    ps, pr = self.ps_half()
                    for hh in range(2):
                        sl = slice(hh * 64, hh * 64 + 64)
                        self.mm(ps[sl, 0:64], X[sl, 0:64], ABK[sl, 64:128], [X.r, ABK.r], pr, inc=(hh == 1))
                    self.tt("dve", RpT[:], ps[:, 0:64], AR[:, n, 64:128], ALU.add, pr + [AR.r], [RpT.r])
                    ps, pr = self.ps_half()
                    for hh in range(2):
                        sl = slice(hh * 64, hh * 64 + 64)
                        self.mm(ps[sl, 0:64], X[sl, 0:64], TR[sl, 128:192], [X.r, TR.r], pr, inc=(hh == 1))
                    self.stt(MT[:], self.id64, g[:, n * 64 + 63:n * 64 + 64], ps[:, 0:64], ALU.mult, ALU.add,
                             pr + [self.cst.r, g.r], [MT.r])
                    ps, pr = self.ps_half()
                    for hh in range(2):
                        sl = slice(hh * 64, hh * 64 + 64)
                        self.mm(ps[sl, 0:64], TR[sl, 128:192], X[sl, 64:128], [TR.r, X.r], pr, start=True, stop=False, inc=False)
                        self.mm(ps[sl, 0:64], TR[sl, 192:256], TR[sl, 64:128], [TR.r], pr, start=False, stop=True, inc=(hh == 1))
                    self.cp("act", Hadd[:], ps[:, 0:64], pr, [Hadd.r])
                    H0, H1 = Hst[hcur], Hst[1 - hcur]
                    hcur = 1 - hcur
                    ps, pr = self.ps_half()
                    for hh in range(2):
                        sl = slice(hh * 64, hh * 64 + 64)
                        self.mm(ps[sl, 0:64], H0[sl, :], RpT[sl, :], [H0.r, RpT.r], pr, start=True, stop=False, inc=False)
                        self.mm(ps[sl, 0:64], X[sl, 64:128], ABK[sl, 64:128], [X.r, ABK.r], pr, start=False, stop=False, inc=False)
                        self.mm(ps[sl, 0:64], TR[sl, 64:128], ABK[sl, 192:256], [TR.r, ABK.r], pr, start=False, stop=True, inc=(hh == 1))
                    self.cp("act", YT[:, csl], ps[:, 0:64], pr, [YT.r])
                    ps, pr = self.ps_half()
                    for hh in range(2):
                        sl = slice(hh * 64, hh * 64 + 64)
                        self.mm(ps[sl, 0:64], MT[sl, :], H0[sl, :], [MT.r, H0.r], pr, inc=(hh == 1))
                    self.tt("dve", H1[:], ps[:, 0:64], Hadd[:], ALU.add, pr + [Hadd.r], [H1.r])

                self.cp("pool", yb[:], YT[:], [YT.r], [yb.r])
                ps, pr = self.ps_full()
                self.mm(ps, self.onesbd[:], yb[:], [self.onesbd.r, yb.r], pr)
                self.stt(yc[:], ps, -1.0 / 64, YT[:], ALU.mult, ALU.add, pr + [YT.r], [yc.r])
                self.act(sqb[:], yc[:], AF.Square, [yc.r], [sqb.r])
                ps, pr = self.ps_full()
                self.mm(ps, self.onesbd[:], sqb[:], [self.onesbd.r, sqb.r], pr)
                self.act(t1[:], ps, AF.Ln, pr, [t1.r], scale=1.0 / 64, bias=GN_EPS)
                self.act(t1[:], t1[:], AF.Exp, [t1.r], [t1.r], scale=-0.5)
                self.tt("dve", yc[:], yc[:], t1[:], ALU.mult, [yc.r, t1.r], [yc.r])
                self.ts("dve", yc[:], yc[:], self.par(P_GNW + hp), ALU.mult, [yc.r, self.params.r], [yc.r],
                        s2=self.par(P_GNB + hp), op1=ALU.add)
                self.tt("pool", yc[:], yc[:], bon[:], ALU.add, [yc.r, bon.r], [yc.r])
                self.tt("dve", BIG[:, 4 + hp, tsl], yc[:], gate[:], ALU.mult, [yc.r, gate.r], [self.bigR[4 + hp]])
        if self.dbg:
            for c in range(8):
                S.dma("pool", self.dbg_o[s, c, :, :], BIG[:, c, :], reads=[self.bigR[c]])

    def phaseD(self, s):
        T, NTB = self.T, self.NTB
        self.phase()
        self.alloc_w()
        BIG = self.BIG
        s0 = [self.alloc("s0", [128, 512], F32) for _ in range(2)]
        s1 = [self.alloc("s1", [128, 512], F32) for _ in range(2)]
        m0 = [self.alloc("m0", [128, 512], F32) for _ in range(2)]
        wbr = [self.alloc("wbr", [128, 4, 128], BF16) for _ in range(4)]
        wg = [self.alloc("wg", [128, 8, 128], BF16) for _ in range(4)]
        it = 0
        for dc in range(8):
            ws = []
            for br in range(2):
                wb = self.load_w(self.w_in_chunk(G0 + br * D + dc * 128, 128), 8, 128)
                w_ = wg[(dc * 2 + br) % 4]
                self.cp("dve", w_[:], wb[:], [wb.r], [w_.r])
                src = self.w_br[br].rearrange("(kc p) n -> p kc n", p=128)[:, :, dc * 128:(dc + 1) * 128]
                wb2 = self.load_w(src, 4, 128)
                w2 = wbr[(dc * 2 + br) % 4]
                self.cp("dve", w2[:], wb2[:, 0:4, :], [wb2.r], [w2.r])
                ws.append((w_, w2))
            for tb in range(NTB):
                tsl = slice(tb * 512, (tb + 1) * 512)
                b = it % 2
                it += 1
                sig = []
                for br in range(2):
                    w_, w2 = ws[br]
                    ps, pr = self.ps_full()
                    for kc in range(8):
                        self.mm(ps, w_[:, kc, :], self.hT[:, kc, tsl], [w_.r, self.hTr[kc]], pr,
                                start=(kc == 0), stop=(kc == 7), inc=(kc == 7))
                    sg = (s0, s1)[br][b]
                    self.act(sg[:], ps, AF.Sigmoid, pr + [self.params.r], [sg.r], bias=self.par(P_BGATE + br * 8 + dc))
                    sig.append(sg)
                ups = []
                for br in range(2):
                    w_, w2 = ws[br]
                    ps, pr = self.ps_full()
                    for kc in range(4):
                        self.mm(ps, w2[:, kc, :], BIG[:, br * 4 + kc, tsl], [w2.r, self.bigR[br * 4 + kc]], pr,
                                start=(kc == 0), stop=(kc == 3), inc=(kc == 3))
                    ups.append((ps, pr))
                self.tt("dve", m0[b][:], ups[0][0], sig[0][:], ALU.mult, ups[0][1] + [sig[0].r], [m0[b].r])
                self.tt("dve", sig[1][:], ups[1][0], sig[1][:], ALU.mult, ups[1][1] + [sig[1].r], [sig[1].r])
                self.tt("pool", BIG[:, 8 + dc, tsl], m0[b][:], sig[1][:], ALU.add, [m0[b].r, sig[1].r], [self.bigR[8 + dc]])
        if self.dbg:
            for c in range(8):
                self.S.dma("pool", self.dbg_m[s, c, :, :], BIG[:, 8 + c, :], reads=[self.bigR[8 + c]])

    def phaseE(self, s):
        S = self.S
        self.phase()
        BIG = self.BIG
        xin = [self.alloc("xin", [128, D], F32) for _ in range(2)]
        x1 = [self.alloc("x1", [128, D], F32) for _ in range(2)]
        xn = [self.alloc("xn", [128, D], F32) for _ in range(1)]
        xnb = [self.alloc("xnb", [128, D], BF16) for _ in range(2)]
        ss = [self.alloc("ss", [128, 1], F32) for _ in range(2)]
        rs = [self.alloc("rs", [128, 1], F32) for _ in range(2)]
        gB = self.alloc("gB", [128, D], F32)
        S.dma("sp", gB[:], self.gbc[1, :, :], writes=[gB.r])
        for i in range(self.NT):
            b = i % 2
            S.dma("sp", xin[b][:], self.x[s, i * 128:(i + 1) * 128, :], writes=[xin[b].r])
            for hf in range(2):
                ps, pr = self.ps_full()
                for kc in range(8):
                    self.mm(ps, BIG[:, 8 + kc, i * 128:(i + 1) * 128], self.wout[:, kc, hf * 512:(hf + 1) * 512],
                            [self.bigR[8 + kc], self.wout.r], pr, start=(kc == 0), stop=(kc == 7), inc=(kc == 7))
                self.tt("dve", x1[b][:, hf * 512:(hf + 1) * 512], ps, xin[b][:, hf * 512:(hf + 1) * 512], ALU.add,
                        pr + [xin[b].r], [x1[b].r])
            S.dma("sp", self.out[s, i * 128:(i + 1) * 128, :], x1[b][:], reads=[x1[b].r], writes=[self.outR[s][i]])
            self.norm_transpose(x1[b], xn[0], xnb[b], ss[b], rs[b], gB, i * 128)

    def phaseF(self, s, hf):
        S = self.S
        T = self.T
        TH = T // 2
        nb = TH // 512
        self.phase()
        self.alloc_w()
        self.uid += 1
        GT = self.nc.alloc_sbuf_tensor_at("GT_%d" % self.uid, [128, NFC, TH], BF16, offset=self.big_off)
        self.GT = GT
        self.gtR = [Res() for _ in range(NFC)]
        ub = [[self.alloc("ub", [128, 514], F32) for _ in range(2)] for _ in range(2)]
        ya = [self.alloc("ya", [128, 512], F32) for _ in range(2)]
        yb = [self.alloc("yb", [128, 512], F32) for _ in range(2)]
        for fc in range(NFC):
            wbs = [self.load_w(self.w_up.rearrange("(kc p) n -> p kc n", p=128)[:, :, gv * DFF + fc * 128: gv * DFF + (fc + 1) * 128], 8, 128)
                   for gv in range(2)]
            for tbl in range(nb):
                t0 = hf * TH + tbl * 512
                res = []
                for gv in range(2):
                    col = gv * NFC + fc
                    cur, nxt = ub[gv][tbl % 2], ub[gv][1 - tbl % 2]
                    ps, pr = self.ps_full()
                    for kc in range(8):
                        self.mm(ps, wbs[gv][:, kc, :], self.hT[:, kc, t0:t0 + 512], [wbs[gv].r, self.hTr[kc]], pr,
                                start=(kc == 0), stop=(kc == 7), inc=(kc == 7))
                    if tbl == 0:
                        if hf == 0:
                            S.op("pool", lambda e, cur=cur: e.memset(cur[:, 0:2], 0.0), [], [cur.r])
                        else:
                            ps2, pr2 = self.ps_half()
                            for kc in range(8):
                                self.mm(ps2[:, 0:2], wbs[gv][:, kc, :], self.hT[:, kc, t0 - 2:t0], [wbs[gv].r, self.hTr[kc]], pr2,
                                        start=(kc == 0), stop=(kc == 7), inc=(kc == 7))
                            self.cp("dve", cur[:, 0:2], ps2[:, 0:2], pr2, [cur.r])
                    self.act(cur[:, 2:514], ps, AF.Copy, pr, [cur.r])
                    self.cp("pool", nxt[:, 0:2], cur[:, 512:514], [cur.r], [nxt.r])
                    y = (ya, yb)[gv][tbl % 2]
                    self.act(y[:], cur[:, 2:514], AF.Identity, [cur.r, self.params.r], [y.r],
                             scale=self.par(P_CW2 + col), bias=self.par(P_CB + col))
                    self.stt(y[:], cur[:, 1:513], self.par(P_CW1 + col), y[:], ALU.mult, ALU.add, [cur.r, y.r, self.params.r], [y.r])
                    self.stt(y[:], cur[:, 0:512], self.par(P_CW0 + col), y[:], ALU.mult, ALU.add, [cur.r, y.r, self.params.r], [y.r])
                    res.append(y)
                self.act(res[0][:], res[0][:], AF.Silu, [res[0].r], [res[0].r])
                self.tt("pool", GT[:, fc, tbl * 512:(tbl + 1) * 512], res[0][:], res[1][:], ALU.mult, [res[0].r, res[1].r], [self.gtR[fc]])

    def phaseG(self, s, hf):
        S = self.S
        T = self.T
        TH = T // 2
        self.S.barrier()
        self.s_off = self.s_base
        GT = self.GT
        wd = self.alloc("wd", [128, NFC, 1024], BF16)
        wst = [self.alloc("wdst", [128, NFC, 128], F32) for _ in range(2)]
        x1 = [self.alloc("x1", [128, D], F32) for _ in range(2)]
        for c in range(8):
            st = wst[c % 2]
            wsrc = self.w_dn.rearrange("(kc p) n -> p kc n", p=128)
            S.dma("sp", st[:, 0:8, :], wsrc[:, 0:8, c * 128:(c + 1) * 128], writes=[st.r])
            S.dma("sp", st[:, 8:16, :], wsrc[:, 8:16, c * 128:(c + 1) * 128], writes=[st.r])
            S.dma("sp", st[:, 16:NFC, :], wsrc[:, 16:NFC, c * 128:(c + 1) * 128], writes=[st.r])
            self.cp("pool" if c % 2 else "dve", wd[:, :, c * 128:(c + 1) * 128], st[:], [st.r], [wd.r])
        for il in range(TH // 128):
            i = hf * (TH // 128) + il
            b = il % 2
            S.dma("sp", x1[b][:], self.out[s, i * 128:(i + 1) * 128, :], reads=[self.outR[s][i]], writes=[x1[b].r])
            for h2 in range(2):
                ps, pr = self.ps_full()
                for fc in range(NFC):
                    self.mm(ps, GT[:, fc, il * 128:(il + 1) * 128], wd[:, fc, h2 * 512:(h2 + 1) * 512], [self.gtR[fc], wd.r], pr,
                            start=(fc == 0), stop=(fc == NFC - 1), inc=(fc == NFC - 1))
                self.tt("dve", x1[b][:, h2 * 512:(h2 + 1) * 512], ps, x1[b][:, h2 * 512:(h2 + 1) * 512], ALU.add,
                        pr + [x1[b].r], [x1[b].r])
            S.dma("sp", self.out[s, i * 128:(i + 1) * 128, :], x1[b][:], reads=[x1[b].r], writes=[self.outR[s][i]])


def make_consts():
    c = np.zeros((128, NCST), np.float32)
    c[:, C_ID:C_ID + 128] = np.eye(128, dtype=np.float32)
    t = np.arange(128)[:, None]
    sk = np.arange(128)[None, :]
    c[:, C_MASK:C_MASK + 128] = np.where(sk < t, 0.0, NEG)
    bd = np.zeros((128, 128), np.float32)
    bd[0:64, 0:64] = 1.0
    bd[64:128, 64:128] = 1.0
    c[:, C_ONESBD:C_ONESBD + 128] = bd
    j = (np.arange(128) % 64)[:, None]
    i = np.arange(64)[None, :]
    strict = (i > j).astype(np.float32)
    incl = (i >= j).astype(np.float32)
    c[:, C_RWMASK:C_RWMASK + 256] = np.concatenate([strict, incl, strict, incl], axis=1)
    c[:, C_ID64:C_ID64 + 64] = np.concatenate([np.eye(64), np.eye(64)], axis=0)
    return c


def pack_params(inp):
    p = np.zeros((128, NPAR), np.float32)

    def fm(vec, col0):
        v = np.asarray(vec, np.float32).reshape(-1)
        n = (v.size + 127) // 128
        vv = np.zeros(n * 128, np.float32)
        vv[:v.size] = v
        p[:, col0:col0 + n] = vv.reshape(n, 128).T

    fm(inp["attn_norm_g"][0], P_ATTN_G)
    fm(inp["ffn_norm_g"][0], P_FFN_G)
    fm(np.tile(np.asarray(inp["q_norm_g"][0]), 2), P_QG)
    fm(np.tile(np.asarray(inp["k_norm_g"][0]), 2), P_KG)
    fm(inp["rwkv_shift_mu"][0], P_MU)
    fm(inp["decay_base"][0], P_DBASE)
    fm(inp["iclr_base"][0], P_IBASE)
    fm(inp["key_k"][0], P_KEYK)
    fm(inp["key_a"][0], P_KEYA)
    fm(inp["group_norm_w"][0], P_GNW)
    fm(inp["group_norm_b"][0], P_GNB)
    fm(inp["bonus_rk"][0], P_BONUS)
    fm(inp["branch_gate_b"][0], P_BGATE)
    cw = np.asarray(inp["ffn_conv_w"][0])
    fm(cw[0], P_CW0)
    fm(cw[1], P_CW1)
    fm(cw[2], P_CW2)
    fm(inp["ffn_conv_b"][0], P_CB)
    return p


def build_program(T, NSEQ, dbg=False):
    kb = KB(T, NSEQ, dbg)
    kb.outR = [[Res() for _ in range(T // 128)] for _ in range(NSEQ)]
    return kb.build()


def make_in_map(inp, xs):
    f = lambda k: np.ascontiguousarray(np.asarray(inp[k], np.float32)[0])
    return {
        "x": np.ascontiguousarray(xs, dtype=np.float32),
        "w_in": f("w_in"), "w_branch": f("w_branch"), "w_out": f("w_out"),
        "w_ffn_up": f("w_ffn_up"), "w_ffn_down": f("w_ffn_down"),
        "decay_up": f("decay_up"), "iclr_up": f("iclr_up"), "out_gate_up": f("out_gate_up"),
        "params": pack_params(inp), "consts": make_consts(),
        "gbc": np.ascontiguousarray(np.stack([np.broadcast_to(np.asarray(inp["attn_norm_g"], np.float32)[0][None, :], (128, D)),
                                              np.broadcast_to(np.asarray(inp["ffn_norm_g"], np.float32)[0][None, :], (128, D))])),
    }


def kernel(**inputs):
    x = np.asarray(inputs["x"], np.float32)
    B, T, _ = x.shape
    ncores = 8
    nseq = B // ncores
    nc = build_program(T, nseq)
    base = make_in_map(inputs, x[0:nseq])
    in_maps = []
    for c in range(ncores):
        m = dict(base)
        m["x"] = np.ascontiguousarray(x[c * nseq:(c + 1) * nseq])
        in_maps.append(m)
    res = run_bass_kernel_spmd(nc, in_maps, core_ids=list(range(ncores)))
    return np.concatenate([np.asarray(r["out"], np.float32) for r in res.results], axis=0)
```
